# Optimizing a Trainium2 kernel written in Bass

```python
import jax, jax.numpy as jnp
from jax import lax
import numpy as np

D_MODEL = 1024
BATCH = 2
SEQ = 8192
DEPTH = 2
DEC_BATCH = 32
DEC_SEQ = 4
PAST_LEN = 16384
PAGE_SIZE = 128

CHUNK = 128
D_A = D_MODEL // 2
G_A = 8
GD_A = D_A // G_A
N_HEADS = 8
HEAD_DIM = 64
N_KV_HEADS = 4
N_IDX_HEADS = 8
IDX_DIM = 64
TOPK_MAX = 256
QBLK = 128
ROPE_THETA = 500000.0
D_FF = 2816
N_SUB = 3
EPS = 1e-6

SPLIT_SIZES = (D_A, D_A, N_HEADS * HEAD_DIM, N_KV_HEADS * HEAD_DIM, N_KV_HEADS * HEAD_DIM,
               N_IDX_HEADS * IDX_DIM, IDX_DIM, N_IDX_HEADS, D_MODEL, D_MODEL)
D_IN = int(sum(SPLIT_SIZES))
SPLIT_AT = tuple(int(s) for s in np.cumsum(SPLIT_SIZES)[:-1])

kernel_name = 'hybrid_chunkmlp_dsa_macaron_step'


def rmsnorm(x, g):
    x32 = x.astype(jnp.float32)
    y = x32 * lax.rsqrt(jnp.mean(x32 * x32, axis=-1, keepdims=True) + EPS)
    return (y * g.astype(jnp.float32)).astype(x.dtype)


def rope(x, pos):
    rd = x.shape[-1] // 4
    half = rd // 2
    freqs = ROPE_THETA ** (-jnp.arange(half, dtype=jnp.float32) * 2.0 / rd)
    ang = pos.astype(jnp.float32)[:, None] * freqs[None, :]
    cos = jnp.cos(ang)[None, :, None, :]
    sin = jnp.sin(ang)[None, :, None, :]
    xr = x[..., :rd].astype(jnp.float32)
    x1, x2 = xr[..., :half], xr[..., half:]
    rot = jnp.concatenate([x1 * cos - x2 * sin, x2 * cos + x1 * sin], axis=-1).astype(x.dtype)
    return jnp.concatenate([rot, x[..., rd:]], axis=-1)


def swiglu(h, w_in, w_out):
    a, b = jnp.split(h @ w_in, 2, axis=-1)
    return (jax.nn.silu(a) * b) @ w_out


def gather_rows(x, idx):
    return jax.vmap(lambda xb, ib: xb[ib])(x, idx)


def mixer_project(h, pos, lp):
    B, T = h.shape[:2]
    z = h @ lp['w_in']
    zu, zv, zq, zk, zva, zqi, zki, zwi, zga, zgb = jnp.split(z, SPLIT_AT, axis=-1)
    u = jax.nn.gelu(zu)
    v = rmsnorm(jax.nn.gelu(zv), lp['sgu_norm_g'])
    q = rope(rmsnorm(zq.reshape(B, T, N_HEADS, HEAD_DIM), lp['q_norm_g']), pos)
    k = rope(rmsnorm(zk.reshape(B, T, N_KV_HEADS, HEAD_DIM), lp['k_norm_g']), pos)
    va = zva.reshape(B, T, N_KV_HEADS, HEAD_DIM)
    qi = rope(zqi.reshape(B, T, N_IDX_HEADS, IDX_DIM), pos)
    ki = rope(zki[:, :, None, :], pos)[:, :, 0, :]
    wi = zwi * (N_IDX_HEADS ** -0.5)
    return u, v, q, k, va, qi, ki, wi, jax.nn.sigmoid(zga), jax.nn.sigmoid(zgb)


def chunk_spatial_gate(u, v, w_s, b_s):
    B, T, _ = v.shape
    r = min(T, CHUNK)
    nc = T // r
    wm = jnp.tril(w_s[:, :r, :r])
    vc = v.reshape(B, nc, r, G_A, GD_A)
    mixed = jnp.einsum('gts,bcsgd->bctgd', wm, vc) + b_s[:, :r].T[None, None, :, :, None]
    return u * mixed.reshape(B, T, D_A).astype(u.dtype)


def index_scores(qi, wi, ki):
    s = jax.nn.relu(jnp.einsum('bthd,bld->bthl', qi, ki).astype(jnp.float32))
    return jnp.einsum('bth,bthl->btl', wi.astype(jnp.float32), s)


def sparse_attend(q, ks, vs, valid):
    B, T = q.shape[:2]
    qg = q.reshape(B, T, N_KV_HEADS, N_HEADS // N_KV_HEADS, HEAD_DIM)
    s = jnp.einsum('btkgd,btskd->btkgs', qg, ks).astype(jnp.float32) * (HEAD_DIM ** -0.5)
    s = jnp.where(valid[:, :, None, None, :], s, -jnp.inf)
    p = jax.nn.softmax(s, axis=-1).astype(vs.dtype)
    o = jnp.einsum('btkgs,btskd->btkgd', p, vs)
    return o.reshape(B, T, N_HEADS * HEAD_DIM)


def prompt_attention(q, k, va, qi, ki, wi):
    B, S = q.shape[:2]
    nb = S // QBLK
    topk = min(TOPK_MAX, S // 4)
    key_pos = jnp.arange(S)

    def to_blocks(a):
        return a.reshape((B, nb, QBLK) + a.shape[2:]).swapaxes(0, 1)

    def block(args):
        i, qb, qib, wib = args
        t = i * QBLK + jnp.arange(QBLK)
        sc = index_scores(qib, wib, ki)
        sc = jnp.where((key_pos[None, :] <= t[:, None])[None], sc, -jnp.inf)
        _, idx = lax.top_k(sc, topk)
        valid = idx <= t[None, :, None]
        return sparse_attend(qb, gather_rows(k, idx), gather_rows(va, idx), valid)

    o = lax.map(block, (jnp.arange(nb), to_blocks(q), to_blocks(qi), to_blocks(wi)))
    return o.swapaxes(0, 1).reshape(B, S, N_HEADS * HEAD_DIM)


def sample_attention(q, k, va, qi, ki, wi, ck, cv, cik, page_table):
    DB, T = q.shape[:2]
    L = PAST_LEN + T
    topk = min(TOPK_MAX, L // 4)
    ki_past = cik[page_table].reshape(DB, PAST_LEN, IDX_DIM)
    ki_all = jnp.concatenate([ki_past, ki.astype(ki_past.dtype)], axis=1)
    t = PAST_LEN + jnp.arange(T)
    sc = index_scores(qi, wi, ki_all)
    sc = jnp.where((jnp.arange(L)[None, :] <= t[:, None])[None], sc, -jnp.inf)
    _, idx = lax.top_k(sc, topk)
    valid = idx <= t[None, :, None]
    s_past = jnp.minimum(idx, PAST_LEN - 1)
    phys = jax.vmap(lambda pt, i: pt[i])(page_table, s_past // PAGE_SIZE)
    off = s_past % PAGE_SIZE
    is_new = (idx >= PAST_LEN)[..., None, None]
    s_new = jnp.clip(idx - PAST_LEN, 0, T - 1)
    ks = jnp.where(is_new, gather_rows(k, s_new).astype(ck.dtype), ck[phys, off])
    vs = jnp.where(is_new, gather_rows(va, s_new).astype(cv.dtype), cv[phys, off])
    return sparse_attend(q, ks, vs, valid)


def merge_branches(o_a, o_b, ga, gb, lp):
    return (ga * (o_a @ lp['w_branch_a']) + gb * (o_b @ lp['w_branch_b'])) @ lp['w_out']


def mix_prompt(h, lp):
    pos = jnp.arange(h.shape[1])
    u, v, q, k, va, qi, ki, wi, ga, gb = mixer_project(h, pos, lp)
    o_a = chunk_spatial_gate(u, v, lp['sgu_w'], lp['sgu_b'])
    o_b = prompt_attention(q, k, va, qi, ki, wi)
    return merge_branches(o_a, o_b, ga, gb, lp), (k, va, ki)


def mix_sample(h, lp, ck, cv, cik, page_table):
    pos = PAST_LEN + jnp.arange(h.shape[1])
    u, v, q, k, va, qi, ki, wi, ga, gb = mixer_project(h, pos, lp)
    o_a = chunk_spatial_gate(u, v, lp['sgu_w'], lp['sgu_b'])
    o_b = sample_attention(q, k, va, qi, ki, wi, ck, cv, cik, page_table)
    return merge_branches(o_a, o_b, ga, gb, lp), (k, va, ki, v)


def layer(x, c, lp, mixer):
    mod = jax.nn.silu(c) @ lp['mod_w'] + lp['mod_b']
    mod = mod.reshape(c.shape[0], N_SUB, 3, D_MODEL)[:, :, :, None, :]

    def adaln(y, j):
        return rmsnorm(y, lp['norm_g'][j]) * (1.0 + mod[:, j, 1]) + mod[:, j, 0]

    x = x + 0.5 * mod[:, 0, 2] * swiglu(adaln(x, 0), lp['ffn_w_in'][0], lp['ffn_w_out'][0])
    y, state = mixer(adaln(x, 1))
    x = x + mod[:, 1, 2] * y
    x = x + 0.5 * mod[:, 2, 2] * swiglu(adaln(x, 2), lp['ffn_w_in'][1], lp['ffn_w_out'][1])
    return x, state


def setup_inputs(seed: int = 0) -> dict:
    key = jax.random.key(seed)
    ks = jax.random.split(key, 24)
    n_pages = PAST_LEN // PAGE_SIZE
    n_used = DEC_BATCH * n_pages
    n_phys = n_used + max(1, n_used // 4)

    def nrm(k, shape, scale):
        return jax.random.normal(k, shape, jnp.float32) * scale

    page_table = jax.random.permutation(ks[7], n_phys)[:n_used].reshape(DEC_BATCH, n_pages).astype(jnp.int32)
    return {
        'x_prompt': nrm(ks[0], (BATCH, SEQ, D_MODEL), 1.0),
        'x_sample': nrm(ks[1], (DEC_BATCH, DEC_SEQ, D_MODEL), 1.0),
        'c_prompt': nrm(ks[2], (BATCH, D_MODEL), 1.0),
        'c_sample': nrm(ks[3], (DEC_BATCH, D_MODEL), 1.0),
        'cache_k': nrm(ks[4], (DEPTH, n_phys, PAGE_SIZE, N_KV_HEADS, HEAD_DIM), 1.0),
        'cache_v': nrm(ks[5], (DEPTH, n_phys, PAGE_SIZE, N_KV_HEADS, HEAD_DIM), 1.0),
        'cache_idx_k': nrm(ks[6], (DEPTH, n_phys, PAGE_SIZE, IDX_DIM), 1.0),
        'page_table': page_table,
        'mod_w': nrm(ks[8], (DEPTH, D_MODEL, N_SUB * 3 * D_MODEL), 0.5 * D_MODEL ** -0.5),
        'mod_b': nrm(ks[9], (DEPTH, N_SUB * 3 * D_MODEL), 0.01),
        'norm_g': 1.0 + nrm(ks[10], (DEPTH, N_SUB, D_MODEL), 0.02),
        'ffn_w_in': nrm(ks[11], (DEPTH, 2, D_MODEL, 2 * D_FF), D_MODEL ** -0.5),
        'ffn_w_out': nrm(ks[12], (DEPTH, 2, D_FF, D_MODEL), D_FF ** -0.5),
        'w_in': nrm(ks[13], (DEPTH, D_MODEL, D_IN), D_MODEL ** -0.5),
        'sgu_norm_g': 1.0 + nrm(ks[14], (DEPTH, D_A), 0.02),
        'sgu_w': nrm(ks[15], (DEPTH, G_A, CHUNK, CHUNK), CHUNK ** -0.5),
        'sgu_b': 1.0 + nrm(ks[16], (DEPTH, G_A, CHUNK), 0.02),
        'q_norm_g': 1.0 + nrm(ks[17], (DEPTH, HEAD_DIM), 0.02),
        'k_norm_g': 1.0 + nrm(ks[18], (DEPTH, HEAD_DIM), 0.02),
        'w_branch_a': nrm(ks[19], (DEPTH, D_A, D_MODEL), D_A ** -0.5),
        'w_branch_b': nrm(ks[20], (DEPTH, N_HEADS * HEAD_DIM, D_MODEL), (N_HEADS * HEAD_DIM) ** -0.5),
        'w_out': nrm(ks[21], (DEPTH, D_MODEL, D_MODEL), D_MODEL ** -0.5),
    }


def reference(x_prompt, x_sample, c_prompt, c_sample, cache_k, cache_v, cache_idx_k, page_table,
              mod_w, mod_b, norm_g, ffn_w_in, ffn_w_out, w_in, sgu_norm_g, sgu_w, sgu_b,
              q_norm_g, k_norm_g, w_branch_a, w_branch_b, w_out):
    yp, ys = x_prompt, x_sample
    kp_l, vp_l, ip_l, ks_l, vs_l, is_l, cv_l = [], [], [], [], [], [], []
    for l in range(DEPTH):
        lp = {'mod_w': mod_w[l], 'mod_b': mod_b[l], 'norm_g': norm_g[l],
              'ffn_w_in': ffn_w_in[l], 'ffn_w_out': ffn_w_out[l], 'w_in': w_in[l],
              'sgu_norm_g': sgu_norm_g[l], 'sgu_w': sgu_w[l], 'sgu_b': sgu_b[l],
              'q_norm_g': q_norm_g[l], 'k_norm_g': k_norm_g[l],
              'w_branch_a': w_branch_a[l], 'w_branch_b': w_branch_b[l], 'w_out': w_out[l]}
        yp, (kp, vp, ip) = layer(yp, c_prompt, lp, lambda h: mix_prompt(h, lp))
        ys, (kn, vn, inew, cvn) = layer(
            ys, c_sample, lp,
            lambda h: mix_sample(h, lp, cache_k[l], cache_v[l], cache_idx_k[l], page_table))
        kp_l.append(kp); vp_l.append(vp); ip_l.append(ip)
        ks_l.append(kn); vs_l.append(vn); is_l.append(inew); cv_l.append(cvn)
    new_k_prompt = jnp.stack(kp_l)
    new_v_prompt = jnp.stack(vp_l)
    new_idx_k_prompt = jnp.stack(ip_l)
    new_k_sample = jnp.stack(ks_l)
    new_v_sample = jnp.stack(vs_l)
    new_idx_k_sample = jnp.stack(is_l)
    new_chunk_v_sample = jnp.stack(cv_l)
    return (yp, ys, new_k_prompt, new_v_prompt, new_idx_k_prompt,
            new_k_sample, new_v_sample, new_idx_k_sample, new_chunk_v_sample)
```

```python
from collections import defaultdict
from contextlib import ExitStack
import numpy as np
import concourse.bass as bass
import concourse.mybir as mybir
from concourse.bass_utils import run_bass_kernel_spmd

F32 = mybir.dt.float32; BF16 = mybir.dt.bfloat16; I32 = mybir.dt.int32
ALU = mybir.AluOpType; AF = mybir.ActivationFunctionType; AX = mybir.AxisListType

DM = 1024; DFF = 2816; NFC = 22; DEPTH = 2; EPS = 1e-6
NEG = -1.0e30
oQ, oK, oQI, oKI, oWI, oVA, oU, oV, oGA, oGB = 0, 512, 768, 1280, 1344, 1352, 1608, 2120, 2632, 3656
COLMAP = [(oQ, 1024, 512), (oK, 1536, 256), (oQI, 2048, 512), (oKI, 2560, 64), (oWI, 2624, 8),
          (oVA, 1792, 256), (oU, 0, 512), (oV, 512, 512), (oGA, 2632, 1024), (oGB, 3656, 1024)]


class Sched:
    NS = 8

    def __init__(self, nc, stack):
        self.nc = nc
        self.engs = {'pe': nc.tensor, 'dve': nc.vector, 'act': nc.scalar, 'pool': nc.gpsimd, 'sp': nc.sync}
        self.sem = {}
        for e in ['pe', 'dve', 'act', 'pool']:
            self.sem[e] = stack.enter_context(nc.semaphore("sem_" + e))
        for q in ['sp', 'pool']:
            for s in range(self.NS):
                self.sem[(q, s)] = stack.enter_context(nc.semaphore(f"semd_{q}{s}"))
        self.cnt = defaultdict(int)
        self.known = defaultdict(lambda: defaultdict(int))
        self.lastw = {}
        self.readers = defaultdict(dict)
        self.dma_n = defaultdict(int)

    def _deps(self, issuer, venue, r, w):
        need = {}
        for k in r:
            lw = self.lastw.get(k)
            if lw:
                need[lw[0]] = max(need.get(lw[0], 0), lw[1])
        for k in w:
            lw = self.lastw.get(k)
            if lw:
                need[lw[0]] = max(need.get(lw[0], 0), lw[1])
            for v, c in self.readers[k].items():
                need[v] = max(need.get(v, 0), c)
        for v, c in need.items():
            if v == 'pe' and venue == 'pe':
                continue
            if self.known[issuer][v] < c:
                self.engs[issuer].wait_ge(self.sem[v], c * (16 if isinstance(v, tuple) else 1))
                self.known[issuer][v] = c

    def _record(self, venue, c, r, w):
        for k in r:
            self.readers[k][venue] = c
        for k in w:
            self.lastw[k] = (venue, c)
            self.readers[k] = {}

    def op(self, eng, fn, r=(), w=()):
        self._deps(eng, eng, r, w)
        ins = fn(self.engs[eng])
        self.cnt[eng] += 1
        ins.then_inc(self.sem[eng], 1)
        self._record(eng, self.cnt[eng], r, w)

    def dma(self, q, fn, r=(), w=()):
        n = self.dma_n[q]
        self.dma_n[q] += 1
        venue = (q, n % self.NS)
        self._deps(q, venue, r, w)
        ins = fn(self.engs[q])
        self.cnt[venue] += 1
        ins.then_inc(self.sem[venue], 16)
        self._record(venue, self.cnt[venue], r, w)

    def barrier(self, issuers=('pe', 'dve', 'act', 'pool', 'sp')):
        for e in issuers:
            for v, c in list(self.cnt.items()):
                if c > 0 and self.known[e][v] < c:
                    self.engs[e].wait_ge(self.sem[v], c * (16 if isinstance(v, tuple) else 1))
                    self.known[e][v] = c


def build(S, NPG, NPHYS, dbg=False):
    NTP = S // 128
    NT = NTP + 1
    NTOK = S + 128
    TOPK_P = min(256, S // 4)
    LS = NPG * 128 + 4
    TOPK_S = min(256, LS // 4)
    NKB_S = 129
    nc = bass.Bass("TRN2", target_bir_lowering=False)

    def din(name, shape, dt=F32):
        return nc.dram_tensor(name, shape, dt, kind="ExternalInput").ap()

    def dout(name, shape):
        return nc.dram_tensor(name, shape, F32, kind="ExternalOutput").ap()

    def dscr(name, shape, dt):
        return nc.dram_tensor(name, shape, dt).ap()

    xp = din("xp", [S, DM]); xs = din("xs", [16, DM]); cc = din("cc", [5, DM])
    ck = din("ck", [DEPTH, NPHYS, 128 * 256]); cv = din("cv", [DEPTH, NPHYS, 128 * 256])
    cik = din("cik", [DEPTH, NPHYS, 128 * 64]); ptT = din("ptT", [128, 4], I32)
    mod_w = din("mod_w", [DEPTH, DM, 9 * DM]); mod_b = din("mod_b", [DEPTH, 9 * DM])
    norm_g = din("norm_g", [DEPTH, 3, DM])
    ffn_w_in = din("ffn_w_in", [DEPTH, 2, DM, 2 * DFF]); ffn_w_out = din("ffn_w_out", [DEPTH, 2, DFF, DM])
    w_in = din("w_in", [DEPTH, DM, 4680]); sgu_norm_g = din("sgu_norm_g", [DEPTH, 512])
    sgu_w = din("sgu_w", [DEPTH, 8, 128, 128]); sgu_b = din("sgu_b", [DEPTH, 8, 128])
    q_norm_g = din("q_norm_g", [DEPTH, 64]); k_norm_g = din("k_norm_g", [DEPTH, 64])
    w_branch_a = din("w_branch_a", [DEPTH, 512, DM]); w_branch_b = din("w_branch_b", [DEPTH, 512, DM])
    w_out = din("w_out", [DEPTH, DM, DM])
    c_ident = din("c_ident", [128, 128]); c_cosp = din("c_cosp", [S, 8]); c_sinp = din("c_sinp", [S, 8])
    c_coss = din("c_coss", [16, 8]); c_sins = din("c_sins", [16, 8])
    c_tri = din("c_tri", [128, 128])
    c_triu = din("c_triu", [128, 128])
    c_bd = din("c_bd", [16, 16])
    c_madd = din("c_madd", [128, 4, 4])

    yp = dout("yp", [S, DM]); ys = dout("ys", [16, DM])
    nkp = dout("nkp", [DEPTH, S, 256]); nvp = dout("nvp", [DEPTH, S, 256]); nip = dout("nip", [DEPTH, S, 64])
    nks = dout("nks", [DEPTH, 16, 256]); nvs = dout("nvs", [DEPTH, 16, 256]); nis = dout("nis", [DEPTH, 16, 64])
    ncv = dout("ncv", [DEPTH, 16, 512])

    xa = dscr("xa", [NTOK, DM], F32); xb = dscr("xb", [NTOK, DM], F32)
    modp_s = dscr("modp_s", [128, 9 * DM], F32); mods_s = dscr("mods_s", [16, 9 * DM], F32)
    qkT_s = dscr("qkT_s", [64, 21, NTOK], BF16)
    vext_s = dscr("vext_s", [NTOK, 260], BF16)
    wi_s = dscr("wi_s", [NTOK, 8], F32)
    oA_s = dscr("oA_s", [NTOK, 512], BF16)
    gab_s = dscr("gab_s", [NTOK, 2048], BF16)
    obT_s = (nc.dram_tensor("obT_s", [64, 8, NTOK], BF16, kind="ExternalOutput").ap() if dbg else dscr("obT_s", [64, 8, NTOK], BF16))

    def tile_rows(t):
        return (128, t * 128) if t < NTP else (16, S)

    def x_src(l, t, which):
        nt, r0 = tile_rows(t)
        if which == 0:
            if l == 0:
                return (xp[r0:r0 + nt, :], None) if t < NTP else (xs[0:16, :], None)
            return (xa[r0:r0 + nt, :], f"xa{t}")
        if which == 1:
            return (xb[r0:r0 + nt, :], f"xb{t}")
        if which == 2:
            return (xa[r0:r0 + nt, :], f"xa{t}")
        if which == 3:
            if l == DEPTH - 1:
                return (yp[r0:r0 + nt, :], None) if t < NTP else (ys[0:16, :], None)
            return (xa[r0:r0 + nt, :], f"xa{t}")


    with ExitStack() as top, nc.allow_non_contiguous_dma(reason="small strided parameter loads"):
        Sx = Sched(nc, top)
        ps = [top.enter_context(nc.psum_tensor(f"ps{i}", [128, 512], F32)) for i in range(8)]
        psb = [p[:].bitcast(BF16) for p in ps]

        uid = [0]

        def T(st, name, shape, dt=F32):
            uid[0] += 1
            return st.enter_context(nc.sbuf_tensor(f"{name}_u{uid[0]}", shape, dt))

        ident_f = T(top, "ident_f", [128, 128]); ident = T(top, "ident", [128, 128], BF16)
        ones_b = T(top, "ones_b", [128, 128], BF16)
        ones_f = T(top, "ones_f", [128, 128])
        Sx.dma('sp', lambda e: e.dma_start(out=ident_f[:], in_=c_ident[:, :]), w=['ident_f'])
        Sx.op('dve', lambda e: e.tensor_copy(out=ident[:], in_=ident_f[:]), r=['ident_f'], w=['ident'])
        Sx.op('dve', lambda e: e.memset(ones_b[:], 1.0), w=['ones_b'])
        Sx.op('dve', lambda e: e.memset(ones_f[:], 1.0), w=['ones_f'])

        def rstd_from_ss(st_tiles, ss_ap, n, nt, key):
            Sx.op('dve', lambda e: e.tensor_scalar(out=ss_ap, in0=ss_ap, scalar1=1.0 / n, scalar2=EPS,
                                                   op0=ALU.mult, op1=ALU.add), r=[key], w=[key])
            Sx.op('act', lambda e: e.activation(out=ss_ap, in_=ss_ap, func=AF.Sqrt), r=[key], w=[key])
            Sx.op('dve', lambda e: e.reciprocal(out=ss_ap, in_=ss_ap), r=[key], w=[key])

        def load_mod(st, l, j):
            res = {}
            for nm, src, nt in (("p", modp_s, 128), ("s", mods_s, 16)):
                m = T(st, f"mod_{nm}", [128, 3, DM])
                gsc = T(st, f"gsc_{nm}", [128, DM])
                ng = T(st, f"ng_{nm}", [128, DM])
                Sx.dma('sp', lambda e: e.dma_start(out=m[0:nt, :, :], in_=src[0:nt, j * 3 * DM:(j + 1) * 3 * DM]
                                                   .rearrange("p (a d) -> p a d", a=3)), r=[f"modscr_{nm}"], w=[f"mod_{nm}"])
                Sx.dma('sp', lambda e: e.dma_start(out=ng[0:nt, :], in_=norm_g[l, j:j + 1, :].to_broadcast([nt, DM])),
                       w=[f"ng_{nm}"])
                Sx.op('dve', lambda e: e.scalar_tensor_tensor(out=gsc[0:nt, :], in0=m[0:nt, 1, :], scalar=1.0, in1=ng[0:nt, :],
                                                               op0=ALU.add, op1=ALU.mult),
                      r=[f"mod_{nm}", f"ng_{nm}"], w=[f"gsc_{nm}"])
                res[nm] = (m, gsc)
            return res

        def adaln_T(st_bufs, l, t, which, modt, hT_ap, xt, tagx):
            nt, r0 = tile_rows(t)
            nm = "p" if t < NTP else "s"
            m, gsc = modt[nm]
            src, key = x_src(l, t, which)
            Sx.dma('sp', lambda e: e.dma_start(out=xt[0:nt, :], in_=src), r=[key] if key else [], w=[tagx])
            junk, ss, hb = st_bufs
            Sx.op('dve', lambda e: e.memset(ss[:], 0.0), w=['ss'])
            Sx.op('act', lambda e: e.activation(out=junk[0:nt, :], in_=xt[0:nt, :], func=AF.Square, accum_out=ss[0:nt, 0:1]),
                  r=[tagx], w=['junk', 'ss'])
            rstd_from_ss(None, ss[0:nt, 0:1], DM, nt, 'ss')
            Sx.op('dve', lambda e: e.scalar_tensor_tensor(out=junk[0:nt, :], in0=xt[0:nt, :], scalar=ss[0:nt, 0:1], in1=gsc[0:nt, :],
                                                           op0=ALU.mult, op1=ALU.mult), r=[tagx, 'ss', f"gsc_{nm}"], w=['junk'])
            Sx.op('dve', lambda e: e.tensor_tensor(out=hb[0:nt, :], in0=junk[0:nt, :], in1=m[0:nt, 0, :], op=ALU.add),
                  r=['junk', f"mod_{nm}"], w=['hb'])
            for c in range(8):
                Sx.op('pe', lambda e: e.transpose(out=psb[7][:, c * 128:c * 128 + nt], in_=hb[0:nt, c * 128:(c + 1) * 128],
                                                  identity=ident[0:nt, 0:nt]), r=['hb', 'ident'], w=['ps7'])
            Sx.op('act', lambda e: e.copy(out=hT_ap[:, :, 0:nt], in_=psb[7][:, :].rearrange("p (c n) -> p c n", c=8)[:, :, 0:nt]),
                  r=['ps7'], w=['hT'])

        def resid_update(l, t, which_out, xt, tagx, y_ps_list, gate_ap, nt, half_scale, tmp):
            for hh in range(2):
                cs = slice(hh * 512, (hh + 1) * 512)
                Sx.op('dve', lambda e: e.scalar_tensor_tensor(out=tmp[0:nt, cs], in0=y_ps_list[hh][0:nt, :], scalar=half_scale,
                                                               in1=gate_ap[0:nt, cs], op0=ALU.mult, op1=ALU.mult),
                      r=[f"ps{4 + hh}", 'mod_p', 'mod_s'], w=['rtmp'])
                Sx.op('dve', lambda e: e.tensor_tensor(out=xt[0:nt, cs], in0=tmp[0:nt, cs], in1=xt[0:nt, cs], op=ALU.add),
                      r=['rtmp', tagx], w=[tagx])
            dst, key = x_src(l, t, which_out)
            Sx.dma('sp', lambda e: e.dma_start(out=dst, in_=xt[0:nt, :]), r=[tagx], w=[key] if key else [])

        def phase_mod(l):
            with ExitStack() as st:
                cs_f = T(st, "cs_f", [5, DM]); cs_b = T(st, "cs_b", [5, DM], BF16)
                cT = T(st, "cT", [128, 8, 5], BF16)
                sTp = T(st, "sTp", [128, 8, 128], BF16); sTs = T(st, "sTs", [128, 8, 16], BF16)
                mb_f = T(st, "mb_f", [1, 9 * DM]); mb_b = T(st, "mb_b", [1, 9 * DM], BF16)
                wbuf = [T(st, f"mw{i}", [128, 8, 512], BF16) for i in range(2)]
                ob = [T(st, f"mo{i}", [128, 512]) for i in range(2)]
                obs = [T(st, f"mos{i}", [16, 512]) for i in range(2)]
                Sx.dma('sp', lambda e: e.dma_start(out=cs_f[:], in_=cc[:, :]), w=['cs_f'])
                Sx.op('act', lambda e: e.activation(out=cs_b[:], in_=cs_f[:], func=AF.Silu), r=['cs_f'], w=['cs_b'])
                for c in range(8):
                    Sx.op('pe', lambda e: e.transpose(out=psb[7][:, c * 8:c * 8 + 5], in_=cs_b[0:5, c * 128:(c + 1) * 128],
                                                      identity=ident[0:5, 0:5]), r=['cs_b', 'ident'], w=['ps7'])
                Sx.op('dve', lambda e: e.tensor_copy(out=cT[:], in_=psb[7][:, 0:64].rearrange("p (c n) -> p c n", c=8)[:, :, 0:5]),
                      r=['ps7'], w=['cT'])
                Sx.op('dve', lambda e: e.tensor_copy(out=sTp[:], in_=cT[:, :, 0:1].to_broadcast([128, 8, 128])), r=['cT'], w=['sTp'])
                Sx.op('dve', lambda e: e.tensor_copy(out=sTs[:].rearrange("p c (s q) -> p c s q", s=4),
                                                     in_=cT[:, :, 1:5].unsqueeze(3).to_broadcast([128, 8, 4, 4])), r=['cT'], w=['sTs'])
                Sx.dma('sp', lambda e: e.dma_start(out=mb_f[:], in_=mod_b[l:l + 1, :]), w=['mb_f'])
                Sx.op('dve', lambda e: e.tensor_copy(out=mb_b[:], in_=mb_f[:]), r=['mb_f'], w=['mb_b'])
                for ch in range(18):
                    wb = wbuf[ch % 2]; wk = f"mw{ch % 2}"
                    Sx.dma('pool', lambda e: e.dma_start(out=wb[:], in_=mod_w[l, :, ch * 512:(ch + 1) * 512]
                                                         .rearrange("(c p) n -> p c n", p=128)), w=[wk])
                    pp, pq = ps[ch % 2], ps[2 + ch % 2]
                    for c in range(8):
                        Sx.op('pe', lambda e: e.matmul(pp[:, :], lhsT=sTp[:, c, :], rhs=wb[:, c, :], start=(c == 0), stop=False),
                              r=['sTp', wk], w=[f"ps{ch % 2}"])
                    Sx.op('pe', lambda e: e.matmul(pp[:, :], lhsT=ones_b[0:1, 0:128], rhs=mb_b[0:1, ch * 512:(ch + 1) * 512],
                                                   start=False, stop=True), r=['ones_b', 'mb_b'], w=[f"ps{ch % 2}"])
                    for c in range(8):
                        Sx.op('pe', lambda e: e.matmul(pq[0:16, :], lhsT=sTs[:, c, :], rhs=wb[:, c, :], start=(c == 0), stop=False),
                              r=['sTs', wk], w=[f"ps{2 + ch % 2}"])
                    Sx.op('pe', lambda e: e.matmul(pq[0:16, :], lhsT=ones_b[0:1, 0:16], rhs=mb_b[0:1, ch * 512:(ch + 1) * 512],
                                                   start=False, stop=True), r=['ones_b', 'mb_b'], w=[f"ps{2 + ch % 2}"])
                    o1 = ob[ch % 2]; o2 = obs[ch % 2]
                    Sx.op('act', lambda e: e.copy(out=o1[:], in_=pp[:, :]), r=[f"ps{ch % 2}"], w=[f"mo{ch % 2}"])
                    Sx.op('dve', lambda e: e.tensor_copy(out=o2[:], in_=pq[0:16, :]), r=[f"ps{2 + ch % 2}"], w=[f"mos{ch % 2}"])
                    Sx.dma('sp', lambda e: e.dma_start(out=modp_s[:, ch * 512:(ch + 1) * 512], in_=o1[:]), r=[f"mo{ch % 2}"], w=["modscr_p"])
                    Sx.dma('sp', lambda e: e.dma_start(out=mods_s[:, ch * 512:(ch + 1) * 512], in_=o2[:]), r=[f"mos{ch % 2}"], w=["modscr_s"])
            Sx.barrier()

        def phase_ffn(l, f):
            j = 0 if f == 0 else 2
            w_in_which, w_out_which = (0, 1) if f == 0 else (2, 3)
            with ExitStack() as st:
                w1 = T(st, "w1", [128, 8, 2 * DFF], BF16)
                w2 = T(st, "w2", [128, NFC, DM], BF16)
                for c in range(8):
                    Sx.dma('pool', lambda e: e.dma_start(out=w1[:, c, :], in_=ffn_w_in[l, f, c * 128:(c + 1) * 128, :]), w=['w1'])
                Sx.dma('pool', lambda e: e.dma_start(out=w2[:, 0:11, :], in_=ffn_w_out[l, f, 0:11 * 128, :].rearrange("(c p) n -> p c n", p=128)), w=['w2'])
                Sx.dma('pool', lambda e: e.dma_start(out=w2[:, 11:22, :], in_=ffn_w_out[l, f, 11 * 128:22 * 128, :].rearrange("(c p) n -> p c n", p=128)), w=['w2'])
                modt = load_mod(st, l, j)
                xt = T(st, "xt", [128, DM]); junk = T(st, "junk", [128, DM]); ss = T(st, "ss", [128, 1])
                hb = T(st, "hb", [128, DM], BF16); hT = T(st, "hT", [128, 8, 128], BF16)
                sa = T(st, "sa", [128, 128], BF16); gT = T(st, "gT", [128, NFC, 128], BF16)
                rtmp = T(st, "rtmp", [128, DM])
                for t in range(NT):
                    nt, r0 = tile_rows(t)
                    nm = "p" if t < NTP else "s"
                    adaln_T((junk, ss, hb), l, t, w_in_which, modt, hT[:], xt, 'xt')
                    for fc in range(NFC):
                        pa, pb = ps[(2 * fc) % 4], ps[(2 * fc + 1) % 4]
                        ka, kb = f"ps{(2 * fc) % 4}", f"ps{(2 * fc + 1) % 4}"
                        for c in range(8):
                            Sx.op('pe', lambda e: e.matmul(pa[:, 0:nt], lhsT=w1[:, c, fc * 128:(fc + 1) * 128], rhs=hT[:, c, 0:nt],
                                                           start=(c == 0), stop=(c == 7)), r=['w1', 'hT'], w=[ka])
                        for c in range(8):
                            Sx.op('pe', lambda e: e.matmul(pb[:, 0:nt], lhsT=w1[:, c, DFF + fc * 128:DFF + (fc + 1) * 128], rhs=hT[:, c, 0:nt],
                                                           start=(c == 0), stop=(c == 7)), r=['w1', 'hT'], w=[kb])
                        Sx.op('act', lambda e: e.activation(out=sa[:, 0:nt], in_=pa[:, 0:nt], func=AF.Silu), r=[ka], w=['sa'])
                        Sx.op('dve', lambda e: e.tensor_tensor(out=gT[:, fc, 0:nt], in0=sa[:, 0:nt], in1=pb[:, 0:nt], op=ALU.mult),
                              r=['sa', kb], w=['gT'])
                    for hh in range(2):
                        py = ps[4 + hh]
                        for fc in range(NFC):
                            Sx.op('pe', lambda e: e.matmul(py[0:nt, :], lhsT=gT[:, fc, 0:nt], rhs=w2[:, fc, hh * 512:(hh + 1) * 512],
                                                           start=(fc == 0), stop=(fc == NFC - 1)), r=['gT', 'w2'], w=[f"ps{4 + hh}"])
                    m, gsc = modt[nm]
                    resid_update(l, t, w_out_which, xt, 'xt', [ps[4], ps[5]], m[:, 2, :], nt, 0.5, rtmp)
            Sx.barrier()

        def phase_proj(l):
            with ExitStack() as st:
                wp = T(st, "wp", [128, 8, 4680], BF16)
                for (mo, oo, wd) in COLMAP:
                    Sx.dma('pool', lambda e: e.dma_start(out=wp[:, :, mo:mo + wd], in_=w_in[l, :, oo:oo + wd]
                                                         .rearrange("(c p) n -> p c n", p=128)), w=['wp'])
                modt = load_mod(st, l, 1)
                xt = T(st, "xt", [128, DM]); junk = T(st, "junk", [128, DM]); ss = T(st, "ss", [128, 1])
                hb = T(st, "hb", [128, DM], BF16); hT = T(st, "hT", [128, 8, 128], BF16)
                sgn = T(st, "sgn", [128, 512]); qkg = T(st, "qkg", [128, 12, 64])
                Wn = T(st, "Wn", [128, 8, 128]); Wnb = T(st, "Wnb", [128, 8, 128], BF16)
                WT = T(st, "WT", [128, 8, 128], BF16); triu = T(st, "triu", [128, 128])
                WTs_f = T(st, "WTs_f", [16, 8, 16]); WTs = T(st, "WTs", [16, 8, 16], BF16); bd = T(st, "bd", [16, 16])
                bT = T(st, "bT", [128, 8]); bTs = T(st, "bTs", [16, 8])
                cosp = T(st, "cosp", [128, NTP, 8]); sinp = T(st, "sinp", [128, NTP, 8])
                coss = T(st, "coss", [16, 8]); sins = T(st, "sins", [16, 8])
                Sx.dma('sp', lambda e: e.dma_start(out=sgn[:], in_=sgu_norm_g[l:l + 1, :].to_broadcast([128, 512])), w=['sgn'])
                Sx.dma('sp', lambda e: e.dma_start(out=qkg[:, 0:8, :], in_=q_norm_g[l:l + 1, :].unsqueeze(1).to_broadcast([128, 8, 64])), w=['qkg'])
                Sx.dma('sp', lambda e: e.dma_start(out=qkg[:, 8:12, :], in_=k_norm_g[l:l + 1, :].unsqueeze(1).to_broadcast([128, 4, 64])), w=['qkg'])
                Sx.dma('sp', lambda e: e.dma_start(out=Wn[:], in_=sgu_w[l].rearrange("g t s -> t g s")), w=['Wn'])
                Sx.dma('sp', lambda e: e.dma_start(out=triu[:], in_=c_triu[:, :]), w=['triu'])
                Sx.dma('sp', lambda e: e.dma_start(out=bd[:], in_=c_bd[:, :]), w=['bd'])
                Sx.dma('sp', lambda e: e.dma_start(out=bT[:], in_=sgu_b[l].rearrange("g t -> t g")), w=['bT'])
                Sx.dma('sp', lambda e: e.dma_start(out=cosp[:], in_=c_cosp.rearrange("(t p) f -> p t f", p=128)), w=['cosp'])
                Sx.dma('sp', lambda e: e.dma_start(out=sinp[:], in_=c_sinp.rearrange("(t p) f -> p t f", p=128)), w=['sinp'])
                Sx.dma('sp', lambda e: e.dma_start(out=coss[:], in_=c_coss[:, :]), w=['coss'])
                Sx.dma('sp', lambda e: e.dma_start(out=sins[:], in_=c_sins[:, :]), w=['sins'])
                Sx.op('dve', lambda e: e.memset(WTs_f[:], 0.0), w=['WTs_f'])
                for i in range(4):
                    for g in range(8):
                        Sx.dma('sp', lambda e: e.dma_start(out=WTs_f[4 * i:4 * i + 4, g, 4 * i:4 * i + 4],
                                                           in_=sgu_w[l, g, 0:4, 0:4].rearrange("t s -> s t")), r=[], w=['WTs_f'])
                    Sx.dma('sp', lambda e: e.dma_start(out=bTs[4 * i:4 * i + 4, :], in_=sgu_b[l, :, 0:4].rearrange("g t -> t g")), w=['bTs'])
                Sx.op('dve', lambda e: e.tensor_tensor(out=WTs[:], in0=WTs_f[:], in1=bd[:].unsqueeze(1).to_broadcast([16, 8, 16]), op=ALU.mult),
                      r=['WTs_f', 'bd'], w=['WTs'])
                Sx.op('dve', lambda e: e.tensor_copy(out=Wnb[:], in_=Wn[:]), r=['Wn'], w=['Wnb'])
                for g in range(8):
                    Sx.op('pe', lambda e: e.transpose(out=psb[6][:, g * 128:(g + 1) * 128], in_=Wnb[:, g, :], identity=ident[:]),
                          r=['Wnb', 'ident'], w=['ps6'])
                Sx.op('dve', lambda e: e.tensor_tensor(out=WT[:], in0=psb[6][:, :].rearrange("p (g t) -> p g t", g=8),
                                                       in1=triu[:].unsqueeze(1).to_broadcast([128, 8, 128]), op=ALU.mult),
                      r=['ps6', 'triu'], w=['WT'])
                qk = T(st, "qk", [128, 1352]); va = T(st, "va", [128, 256]); ub = T(st, "ub", [128, 512], BF16)
                vg = T(st, "vg", [128, 512]); vf = T(st, "vf", [128, 512]); vb = T(st, "vb", [128, 512], BF16)
                ssv = T(st, "ssv", [128, 1]); mtmp = T(st, "mtmp", [128, 512]); oAb = T(st, "oAb", [128, 512], BF16)
                gab = T(st, "gab", [128, 2048], BF16); sq = T(st, "sq", [128, 768]); ss12 = T(st, "ss12", [128, 12])
                r1 = T(st, "r1", [128, 21, 8]); r2 = T(st, "r2", [128, 21, 8]); r3 = T(st, "r3", [128, 21, 8]); r4 = T(st, "r4", [128, 21, 8])
                qkb = T(st, "qkb", [128, 1344], BF16); TT = T(st, "TT", [64, 21, 128], BF16)
                vext = T(st, "vext", [128, 4, 65], BF16)
                Sx.op('dve', lambda e: e.memset(vext[:], 1.0), w=['vext'])
                Sx.op('dve', lambda e: e.memset(TT[:], 0.0), w=['TT'])
                chunks = [(0, 512), (512, 1024), (1024, 1352), (oVA, oVA + 256), (oU, oU + 512), (oV, oV + 512)]
                for t in range(NT):
                    nt, r0 = tile_rows(t)
                    issamp = t >= NTP
                    nm = "p" if not issamp else "s"
                    adaln_T((junk, ss, hb), l, t, 1, modt, hT[:], xt, 'xt')
                    for ci, (c0, c1) in enumerate(chunks):
                        for c in range(8):
                            Sx.op('pe', lambda e: e.matmul(ps[ci][0:nt, 0:c1 - c0], lhsT=hT[:, c, 0:nt], rhs=wp[:, c, c0:c1],
                                                           start=(c == 0), stop=(c == 7)), r=['hT', 'wp'], w=[f"ps{ci}"])
                    Sx.op('act', lambda e: e.copy(out=qk[0:nt, 0:512], in_=ps[0][0:nt, :]), r=['ps0'], w=['qk'])
                    Sx.op('act', lambda e: e.copy(out=qk[0:nt, 512:1024], in_=ps[1][0:nt, :]), r=['ps1'], w=['qk'])
                    Sx.op('act', lambda e: e.copy(out=qk[0:nt, 1024:1352], in_=ps[2][0:nt, 0:328]), r=['ps2'], w=['qk'])
                    Sx.op('act', lambda e: e.copy(out=va[0:nt, :], in_=ps[3][0:nt, 0:256]), r=['ps3'], w=['va'])
                    Sx.op('act', lambda e: e.activation(out=ub[0:nt, :], in_=ps[4][0:nt, :], func=AF.Gelu_apprx_tanh), r=['ps4'], w=['ub'])
                    Sx.op('act', lambda e: e.activation(out=vg[0:nt, :], in_=ps[5][0:nt, :], func=AF.Gelu_apprx_tanh), r=['ps5'], w=['vg'])
                    for gi in range(4):
                        c0 = oGA + gi * 512
                        for c in range(8):
                            Sx.op('pe', lambda e: e.matmul(ps[gi][0:nt, :], lhsT=hT[:, c, 0:nt], rhs=wp[:, c, c0:c0 + 512],
                                                           start=(c == 0), stop=(c == 7)), r=['hT', 'wp'], w=[f"ps{gi}"])
                        Sx.op('act', lambda e: e.activation(out=gab[0:nt, gi * 512:(gi + 1) * 512], in_=ps[gi][0:nt, :], func=AF.Sigmoid),
                              r=[f"ps{gi}"], w=['gab'])
                    Sx.dma('sp', lambda e: e.dma_start(out=gab_s[r0:r0 + nt, :], in_=gab[0:nt, :]), r=['gab'], w=[f"gab_s{t}"])
                    Sx.op('dve', lambda e: e.memset(ssv[:], 0.0), w=['ssv'])
                    Sx.op('act', lambda e: e.activation(out=mtmp[0:nt, :], in_=vg[0:nt, :], func=AF.Square, accum_out=ssv[0:nt, 0:1]),
                          r=['vg'], w=['mtmp', 'ssv'])
                    rstd_from_ss(None, ssv[0:nt, 0:1], 512, nt, 'ssv')
                    Sx.op('dve', lambda e: e.scalar_tensor_tensor(out=vf[0:nt, :], in0=vg[0:nt, :], scalar=ssv[0:nt, 0:1], in1=sgn[0:nt, :],
                                                                   op0=ALU.mult, op1=ALU.mult), r=['vg', 'ssv', 'sgn'], w=['vf'])
                    Sx.op('dve', lambda e: e.tensor_copy(out=vb[0:nt, :], in_=vf[0:nt, :]), r=['vf'], w=['vb'])
                    if issamp:
                        Sx.dma('sp', lambda e: e.dma_start(out=ncv[l, :, :], in_=vf[0:16, :]), r=['vf'])
                    for g in range(8):
                        lhs = WT[0:nt, g, 0:nt] if not issamp else WTs[0:16, g, 0:16]
                        Sx.op('pe', lambda e: e.matmul(ps[6][0:nt, g * 64:(g + 1) * 64], lhsT=lhs, rhs=vb[0:nt, g * 64:(g + 1) * 64],
                                                       start=True, stop=True), r=['WT', 'WTs', 'vb'], w=['ps6'])
                    bsel = bT if not issamp else bTs
                    Sx.op('dve', lambda e: e.tensor_tensor(out=mtmp[0:nt, :].rearrange("p (g d) -> p g d", g=8),
                                                           in0=ps[6][0:nt, :].rearrange("p (g d) -> p g d", g=8),
                                                           in1=bsel[0:nt, :].unsqueeze(2).to_broadcast([nt, 8, 64]), op=ALU.add),
                          r=['ps6', 'bT', 'bTs'], w=['mtmp'])
                    Sx.op('dve', lambda e: e.tensor_tensor(out=oAb[0:nt, :], in0=mtmp[0:nt, :], in1=ub[0:nt, :], op=ALU.mult),
                          r=['mtmp', 'ub'], w=['oAb'])
                    Sx.dma('sp', lambda e: e.dma_start(out=oA_s[r0:r0 + nt, :], in_=oAb[0:nt, :]), r=['oAb'], w=[f"oA_s{t}"])
                    Sx.op('act', lambda e: e.activation(out=sq[0:nt, :], in_=qk[0:nt, 0:768], func=AF.Square), r=['qk'], w=['sq'])
                    Sx.op('dve', lambda e: e.tensor_reduce(out=ss12[0:nt, :], in_=sq[0:nt, :].rearrange("p (h d) -> p h d", h=12),
                                                           axis=AX.X, op=ALU.add), r=['sq'], w=['ss12'])
                    rstd_from_ss(None, ss12[0:nt, :], 64, nt, 'ss12')
                    q3 = qk[0:nt, 0:768].rearrange("p (h d) -> p h d", h=12)
                    Sx.op('dve', lambda e: e.tensor_tensor(out=q3, in0=q3, in1=ss12[0:nt, :].unsqueeze(2).to_broadcast([nt, 12, 64]), op=ALU.mult),
                          r=['qk', 'ss12'], w=['qk'])
                    Sx.op('dve', lambda e: e.tensor_tensor(out=q3, in0=q3, in1=qkg[0:nt, :, :], op=ALU.mult), r=['qk', 'qkg'], w=['qk'])
                    R = qk[0:nt, 0:1344].rearrange("p (h d) -> p h d", h=21)
                    x1 = R[:, :, 0:8]; x2 = R[:, :, 8:16]
                    if not issamp:
                        cb = cosp[0:nt, t:t + 1, :].to_broadcast([nt, 21, 8]); sb_ = sinp[0:nt, t:t + 1, :].to_broadcast([nt, 21, 8])
                    else:
                        cb = coss[0:nt, :].unsqueeze(1).to_broadcast([nt, 21, 8]); sb_ = sins[0:nt, :].unsqueeze(1).to_broadcast([nt, 21, 8])
                    Sx.op('dve', lambda e: e.tensor_tensor(out=r1[0:nt], in0=x1, in1=cb, op=ALU.mult), r=['qk', 'cosp', 'coss'], w=['r1'])
                    Sx.op('dve', lambda e: e.tensor_tensor(out=r2[0:nt], in0=x2, in1=sb_, op=ALU.mult), r=['qk', 'sinp', 'sins'], w=['r2'])
                    Sx.op('dve', lambda e: e.tensor_tensor(out=r3[0:nt], in0=x2, in1=cb, op=ALU.mult), r=['qk', 'cosp', 'coss'], w=['r3'])
                    Sx.op('dve', lambda e: e.tensor_tensor(out=r4[0:nt], in0=x1, in1=sb_, op=ALU.mult), r=['qk', 'sinp', 'sins'], w=['r4'])
                    Sx.op('dve', lambda e: e.tensor_tensor(out=x1, in0=r1[0:nt], in1=r2[0:nt], op=ALU.subtract), r=['r1', 'r2'], w=['qk'])
                    Sx.op('dve', lambda e: e.tensor_tensor(out=x2, in0=r3[0:nt], in1=r4[0:nt], op=ALU.add), r=['r3', 'r4'], w=['qk'])
                    Sx.op('dve', lambda e: e.tensor_scalar(out=qk[0:nt, 1344:1352], in0=qk[0:nt, 1344:1352], scalar1=8.0 ** -0.5, scalar2=None,
                                                           op0=ALU.mult), r=['qk'], w=['qk'])
                    if not issamp:
                        Sx.dma('sp', lambda e: e.dma_start(out=nkp[l, r0:r0 + nt, :], in_=qk[0:nt, 512:768]), r=['qk'])
                        Sx.dma('sp', lambda e: e.dma_start(out=nip[l, r0:r0 + nt, :], in_=qk[0:nt, 1280:1344]), r=['qk'])
                        Sx.dma('sp', lambda e: e.dma_start(out=nvp[l, r0:r0 + nt, :], in_=va[0:nt, :]), r=['va'])
                    else:
                        Sx.dma('sp', lambda e: e.dma_start(out=nks[l, :, :], in_=qk[0:16, 512:768]), r=['qk'])
                        Sx.dma('sp', lambda e: e.dma_start(out=nis[l, :, :], in_=qk[0:16, 1280:1344]), r=['qk'])
                        Sx.dma('sp', lambda e: e.dma_start(out=nvs[l, :, :], in_=va[0:16, :]), r=['va'])
                    Sx.dma('sp', lambda e: e.dma_start(out=wi_s[r0:r0 + nt, :], in_=qk[0:nt, 1344:1352]), r=['qk'], w=[f"wi_s{t}"])
                    Sx.op('dve', lambda e: e.tensor_copy(out=qkb[0:nt, :], in_=qk[0:nt, 0:1344]), r=['qk'], w=['qkb'])
                    for rr in range(3):
                        nh = 8 if rr < 2 else 5
                        for hh in range(nh):
                            h = rr * 8 + hh
                            Sx.op('pe', lambda e: e.transpose(out=psb[7][0:64, hh * 128:hh * 128 + nt], in_=qkb[0:nt, h * 64:(h + 1) * 64],
                                                              identity=ident[0:nt, 0:nt]), r=['qkb', 'ident'], w=['ps7'])
                        Sx.op('act', lambda e: e.copy(out=TT[:, rr * 8:rr * 8 + nh, 0:nt],
                                                      in_=psb[7][0:64, 0:nh * 128].rearrange("p (h n) -> p h n", h=nh)[:, :, 0:nt]),
                              r=['ps7'], w=['TT'])
                    Sx.dma('sp', lambda e: e.dma_start(out=qkT_s[:, :, r0:r0 + 128], in_=TT[:, :, :]), r=['TT'], w=[f"qkT_s{t}"])
                    Sx.op('dve', lambda e: e.tensor_copy(out=vext[0:nt, :, 0:64], in_=va[0:nt, :].rearrange("p (g d) -> p g d", g=4)),
                          r=['va'], w=['vext'])
                    Sx.dma('sp', lambda e: e.dma_start(out=vext_s[r0:r0 + nt, :], in_=vext[0:nt, :, :].rearrange("p g d -> p (g d)")),
                           r=['vext'], w=[f"vext_s{t}"])
            Sx.barrier()

        def attn_core(st, nq, nkb, kT_of, v_of, qT, mT_of, obT_tile, pT, oacc_sb, rec, kvalid_of):
            W = 2 * nq
            for kb in range(nkb):
                nk = kvalid_of(kb)
                for g in range(4):
                    kap, kk = kT_of(kb, g)
                    bank = ps[g // 2]; off = (g % 2) * 256
                    Sx.op('pe', lambda e: e.matmul(bank[0:nk, off:off + W], lhsT=kap, rhs=qT[:, 2 * g:2 * g + 2, 0:nq],
                                                   start=True, stop=True), r=kk + ['qT'], w=[f"ps{g // 2}"])
                for b2 in range(2):
                    src = ps[b2][0:nk, :].rearrange("p (g w) -> p g w", g=2)[:, :, 0:W]
                    dst = pT[0:nk, 2 * b2:2 * b2 + 2, 0:W]
                    Sx.op('act', lambda e: e.activation(out=dst, in_=src, func=AF.Exp, scale=0.125), r=[f"ps{b2}"], w=['pT'])
                map_, mk = mT_of(kb)
                p4 = pT[0:nk, :, 0:W].rearrange("p g (h q) -> p (g h) q", h=2)
                Sx.op('dve', lambda e: e.tensor_tensor(out=p4, in0=p4, in1=map_.unsqueeze(1).to_broadcast([nk, 8, nq]), op=ALU.mult),
                      r=['pT'] + mk, w=['pT'])
                for g in range(4):
                    vap, vk = v_of(kb, g)
                    bank = ps[2 + g]
                    Sx.op('pe', lambda e: e.matmul(bank[0:65, 0:W], lhsT=vap, rhs=pT[0:nk, g, 0:W],
                                                   start=(kb == 0), stop=(kb == nkb - 1)), r=vk + ['pT'], w=[f"ps{2 + g}"])
            for g in range(4):
                Sx.op('act', lambda e: e.copy(out=oacc_sb[0:65, g, 0:W], in_=ps[2 + g][0:65, 0:W]), r=[f"ps{2 + g}"], w=['oacc'])
            Sx.op('dve', lambda e: e.reciprocal(out=rec[64:65, :, 0:W], in_=oacc_sb[64:65, :, 0:W]), r=['oacc'], w=['rec'])
            for g in range(4):
                bank = ps[6 + g // 2]; off = (g % 2) * 256
                Sx.op('pe', lambda e: e.matmul(bank[0:64, off:off + W], lhsT=ones_f[64:65, 0:64], rhs=rec[64:65, g, 0:W],
                                               start=True, stop=True), r=['ones_f', 'rec'], w=[f"ps{6 + g // 2}"])
            for b2 in range(2):
                src = ps[6 + b2][0:64, :].rearrange("p (g w) -> p g w", g=2)[:, :, 0:W]
                dst = obT_tile[0:64, 4 * b2:4 * b2 + 4, 0:nq].rearrange("p (g h) q -> p g (h q)", g=2)
                Sx.op('dve', lambda e: e.tensor_tensor(out=dst, in0=oacc_sb[0:64, 2 * b2:2 * b2 + 2, 0:W], in1=src, op=ALU.mult),
                      r=['oacc', f"ps{6 + b2}"], w=['obT'])

        def phase_attn_prompt(l):
            with ExitStack() as st:
                kT = T(st, "kT", [64, 4, S], BF16); kiT = T(st, "kiT", [64, S], BF16)
                vx = T(st, "vx", [128, NTP, 260], BF16)
                Sx.dma('sp', lambda e: e.dma_start(out=kT[:], in_=qkT_s[:, 8:12, 0:S]), r=[f"qkT_s{t}" for t in range(NTP)], w=['kT'])
                Sx.dma('sp', lambda e: e.dma_start(out=kiT[:], in_=qkT_s[:, 20, 0:S]), r=[f"qkT_s{t}" for t in range(NTP)], w=['kiT'])
                Sx.dma('sp', lambda e: e.dma_start(out=vx[:], in_=vext_s[0:S, :].rearrange("(t p) c -> p t c", p=128)),
                       r=[f"vext_s{t}" for t in range(NTP)], w=['vx'])
                tri = T(st, "tri", [128, 128])
                Sx.dma('sp', lambda e: e.dma_start(out=tri[:], in_=c_tri[:, :]), w=['tri'])
                qT = T(st, "qT", [64, 8, 128], BF16); qiT = T(st, "qiT", [64, 8, 128], BF16)
                wi = T(st, "wi", [128, 8]); wa = T(st, "wa", [128, 8]); wsg = T(st, "wsg", [128, 8])
                sc = T(st, "sc", [128, S]); rl = T(st, "rl", [128, 512])
                mask = T(st, "mask", [128, S], BF16); mT = T(st, "mT", [128, NTP, 128], BF16)
                lo = T(st, "lo", [128, 1]); w0 = T(st, "w0", [128, 1]); hi = T(st, "hi", [128, 1]); mid = T(st, "mid", [128, 1])
                cnt = T(st, "cnt", [128, 1]); pp = T(st, "pp", [128, 1])
                pT = T(st, "pT", [128, 4, 256], BF16); oacc = T(st, "oacc", [128, 4, 256]); rec = T(st, "rec", [128, 4, 256])
                obT = T(st, "obT", [64, 8, 128], BF16)
                for t in range(NTP):
                    nkb = t + 1; nk = nkb * 128
                    r0 = t * 128
                    Sx.dma('sp', lambda e: e.dma_start(out=qT[:], in_=qkT_s[:, 0:8, r0:r0 + 128]), r=[f"qkT_s{t}"], w=['qT'])
                    Sx.dma('sp', lambda e: e.dma_start(out=qiT[:], in_=qkT_s[:, 12:20, r0:r0 + 128]), r=[f"qkT_s{t}"], w=['qiT'])
                    Sx.dma('sp', lambda e: e.dma_start(out=wi[:], in_=wi_s[r0:r0 + 128, :]), r=[f"wi_s{t}"], w=['wi'])
                    Sx.op('dve', lambda e: e.scalar_tensor_tensor(out=wa[:], in0=wi[:], scalar=-1.0, in1=wi[:], op0=ALU.mult, op1=ALU.max), r=['wi'], w=['wa'])
                    Sx.op('act', lambda e: e.activation(out=wsg[:], in_=wi[:], func=AF.Sign), r=['wi'], w=['wsg'])
                    for c0 in range(0, nk, 512):
                        cw = min(512, nk - c0)
                        for h in range(8):
                            bank = ps[h % 2]
                            Sx.op('pe', lambda e: e.matmul(bank[:, 0:cw], lhsT=qiT[:, h, :], rhs=kiT[:, c0:c0 + cw], start=True, stop=True),
                                  r=['qiT', 'kiT'], w=[f"ps{h % 2}"])
                            if h == 0:
                                Sx.op('act', lambda e: e.activation(out=rl[:, 0:cw], in_=bank[:, 0:cw], func=AF.Relu, scale=wa[:, h:h + 1]),
                                      r=[f"ps{h % 2}", 'wa'], w=['rl'])
                                Sx.op('dve', lambda e: e.tensor_scalar(out=sc[:, c0:c0 + cw], in0=rl[:, 0:cw], scalar1=wsg[:, 0:1], scalar2=None,
                                                                       op0=ALU.mult), r=['rl', 'wsg'], w=['sc'])
                            else:
                                Sx.op('act', lambda e: e.activation(out=rl[:, 0:cw], in_=bank[:, 0:cw], func=AF.Relu, scale=wa[:, h:h + 1]),
                                      r=[f"ps{h % 2}", 'wa'], w=['rl'])
                                Sx.op('dve', lambda e: e.scalar_tensor_tensor(out=sc[:, c0:c0 + cw], in0=rl[:, 0:cw], scalar=wsg[:, h:h + 1],
                                                                               in1=sc[:, c0:c0 + cw], op0=ALU.mult, op1=ALU.add),
                                      r=['rl', 'wsg', 'sc'], w=['sc'])
                    if nk > TOPK_P:
                        Sx.op('dve', lambda e: e.tensor_reduce(out=hi[:], in_=sc[:, 0:nk], axis=AX.X, op=ALU.max), r=['sc'], w=['hi'])
                        Sx.op('dve', lambda e: e.tensor_reduce(out=lo[:], in_=sc[:, 0:nk], axis=AX.X, op=ALU.min), r=['sc'], w=['lo'])
                        Sx.op('dve', lambda e: e.tensor_tensor(out=w0[:], in0=hi[:], in1=lo[:], op=ALU.subtract), r=['hi', 'lo'], w=['w0'])
                    Sx.op('dve', lambda e: e.tensor_tensor(out=sc[:, nk - 128:nk], in0=sc[:, nk - 128:nk], in1=tri[:], op=ALU.add),
                          r=['sc', 'tri'], w=['sc'])
                    if nk > TOPK_P:
                        for it in range(1, 27):
                            f = 2.0 ** (-it)
                            Sx.op('dve', lambda e: e.scalar_tensor_tensor(out=mid[:], in0=w0[:], scalar=f, in1=lo[:], op0=ALU.mult, op1=ALU.add),
                                  r=['w0', 'lo'], w=['mid'])
                            Sx.op('dve', lambda e: e.memset(cnt[:], 0.0), w=['cnt'])
                            Sx.op('dve', lambda e: e.tensor_scalar(out=mask[:, 0:nk], in0=sc[:, 0:nk], scalar1=mid[:, 0:1], scalar2=0.0,
                                                                   op0=ALU.is_ge, op1=ALU.add, accum_out=cnt[:, 0:1]),
                                  r=['sc', 'mid'], w=['mask', 'cnt'])
                            Sx.op('dve', lambda e: e.tensor_scalar(out=pp[:], in0=cnt[:], scalar1=TOPK_P - 0.5, scalar2=f, op0=ALU.is_ge, op1=ALU.mult),
                                  r=['cnt'], w=['pp'])
                            Sx.op('dve', lambda e: e.scalar_tensor_tensor(out=lo[:], in0=pp[:], scalar=w0[:, 0:1], in1=lo[:], op0=ALU.mult, op1=ALU.add),
                                  r=['pp', 'w0', 'lo'], w=['lo'])
                    else:
                        Sx.op('dve', lambda e: e.memset(lo[:], -1.0e29), w=['lo'])
                    Sx.op('dve', lambda e: e.tensor_scalar(out=mask[:, 0:nk], in0=sc[:, 0:nk], scalar1=lo[:, 0:1], scalar2=None, op0=ALU.is_ge),
                          r=['sc', 'lo'], w=['mask'])
                    for b0 in range(0, nkb, 8):
                        nb = min(8, nkb - b0)
                        for bb in range(nb):
                            Sx.op('pe', lambda e: e.transpose(out=psb[6][:, bb * 128:(bb + 1) * 128], in_=mask[:, (b0 + bb) * 128:(b0 + bb + 1) * 128],
                                                              identity=ident[:]), r=['mask', 'ident'], w=['ps6'])
                        Sx.op('act', lambda e: e.copy(out=mT[:, b0:b0 + nb, :], in_=psb[6][:, 0:nb * 128].rearrange("p (b q) -> p b q", b=nb)),
                              r=['ps6'], w=['mT'])
                    attn_core(st, 128, nkb,
                              lambda kb, g: (kT[:, g, kb * 128:(kb + 1) * 128], ['kT']),
                              lambda kb, g: (vx[:, kb, g * 65:(g + 1) * 65], ['vx']),
                              qT, lambda kb: (mT[:, kb, :], ['mT']), obT, pT, oacc, rec, lambda kb: 128)
                    Sx.dma('sp', lambda e: e.dma_start(out=obT_s[:, :, r0:r0 + 128], in_=obT[:]), r=['obT'], w=[f"obT_s{t}"])
            Sx.barrier()

        def phase_attn_sample(l):
            NP = NPG
            with ExitStack() as st:
                pts = T(st, "pts", [128, 4], I32)
                Sx.dma('sp', lambda e: e.dma_start(out=pts[:], in_=ptT[:, :]), w=['pts'])
                madd = T(st, "madd", [128, 4, 4])
                idx8 = T(st, "idx8", [128, 4, 8], I32)
                ptf = T(st, "ptf", [128, 4]); idxf = T(st, "idxf", [128, 4, 8])
                Sx.op('dve', lambda e: e.tensor_copy(out=ptf[:], in_=pts[:]), r=['pts'], w=['ptf'])
                for c8 in range(8):
                    Sx.op('dve', lambda e: e.tensor_scalar(out=idxf[:, :, c8], in0=ptf[:, :], scalar1=8.0, scalar2=float(c8), op0=ALU.mult, op1=ALU.add),
                          r=['ptf'], w=['idxf'])
                Sx.op('dve', lambda e: e.tensor_copy(out=idx8[:], in_=idxf[:]), r=['idxf'], w=['idx8'])
                Sx.dma('sp', lambda e: e.dma_start(out=madd[:], in_=c_madd[:, :, :]), w=['madd'])
                r0 = S
                qT = T(st, "qTs", [64, 8, 16], BF16); qiT = T(st, "qiTs", [64, 8, 16], BF16)
                kTn = T(st, "kTn", [64, 4, 16], BF16); kiTn = T(st, "kiTn", [64, 16], BF16)
                vxn = T(st, "vxn", [16, 260], BF16)
                Sx.dma('sp', lambda e: e.dma_start(out=qT[:], in_=qkT_s[:, 0:8, r0:r0 + 16]), r=[f"qkT_s{NTP}"], w=['qTs'])
                Sx.dma('sp', lambda e: e.dma_start(out=qiT[:], in_=qkT_s[:, 12:20, r0:r0 + 16]), r=[f"qkT_s{NTP}"], w=['qiTs'])
                Sx.dma('sp', lambda e: e.dma_start(out=kTn[:], in_=qkT_s[:, 8:12, r0:r0 + 16]), r=[f"qkT_s{NTP}"], w=['kTn'])
                Sx.dma('sp', lambda e: e.dma_start(out=kiTn[:], in_=qkT_s[:, 20, r0:r0 + 16]), r=[f"qkT_s{NTP}"], w=['kiTn'])
                Sx.dma('sp', lambda e: e.dma_start(out=vxn[:], in_=vext_s[r0:r0 + 16, :]), r=[f"vext_s{NTP}"], w=['vxn'])
                wrow = T(st, "wrow", [1, 16, 8]); wrow_b = T(st, "wrow_b", [1, 4, 8, 4])
                wbc = T(st, "wbc", [128, 4, 8, 4])
                Sx.dma('sp', lambda e: e.dma_start(out=wrow[:], in_=wi_s[r0:r0 + 16, :].rearrange("(o t) h -> o t h", o=1)), r=[f"wi_s{NTP}"], w=['wrow'])
                Sx.op('dve', lambda e: e.tensor_copy(out=wrow_b[:], in_=wrow[:].rearrange("o (s q) h -> o s h q", s=4)), r=['wrow'], w=['wrow_b'])
                Sx.op('pe', lambda e: e.matmul(ps[7][:, 0:128], lhsT=ones_f[0:1, 0:128], rhs=wrow_b[:].rearrange("o s h q -> o (s h q)"),
                                               start=True, stop=True), r=['ones_f', 'wrow_b'], w=['ps7'])
                Sx.op('dve', lambda e: e.tensor_copy(out=wbc[:].rearrange("p s h q -> p (s h q)"), in_=ps[7][:, 0:128]), r=['ps7'], w=['wbc'])
                scT = T(st, "scT", [128, 4, NKB_S, 4])
                Sx.op('dve', lambda e: e.memset(scT[:], NEG), w=['scT'])
                kig = T(st, "kig", [128, 128 * 64]); kib = T(st, "kib", [128, 128 * 64], BF16)
                kiTb = T(st, "kiTb", [64, 16, 128], BF16)
                rs = T(st, "rs", [128, 16, 32]); rs2 = T(st, "rs2", [128, 16, 32])
                for s in range(4):
                    Sx.dma('pool', lambda e: e.indirect_dma_start(out=kig[0:NP, :], out_offset=None,
                                                                  in_=bass.AP(cik.tensor, 0, [[8192, NPHYS], [1, 8192]]),
                                                                  in_offset=bass.IndirectOffsetOnAxis(ap=pts[0:NP, s:s + 1], axis=0),
                                                                  element_offset=l * NPHYS * 8192),
                           r=['pts'], w=['kig'])
                    Sx.op('dve', lambda e: e.tensor_copy(out=kib[0:NP, :], in_=kig[0:NP, :]), r=['kig'], w=['kib'])
                    for j0 in range(0, 128, 16):
                        for jj in range(16):
                            j = j0 + jj
                            Sx.op('pe', lambda e: e.transpose(out=psb[(j // 8) % 2][0:64, (j % 8) * 128:(j % 8) * 128 + NP],
                                                              in_=kib[0:NP, j * 64:(j + 1) * 64], identity=ident[0:NP, 0:NP]),
                                  r=['kib', 'ident'], w=[f"ps{(j // 8) % 2}"])
                            if j % 8 == 7:
                                bnk = (j // 8) % 2
                                Sx.op('act', lambda e: e.copy(out=kiTb[:, jj - 7:jj + 1, 0:NP],
                                                              in_=psb[bnk][0:64, :].rearrange("p (b n) -> p b n", b=8)[:, :, 0:NP]),
                                      r=[f"ps{bnk}"], w=['kiTb'])
                        for jj in range(16):
                            Sx.op('pe', lambda e: e.matmul(ps[2][0:NP, jj * 32:(jj + 1) * 32], lhsT=kiTb[:, jj, 0:NP],
                                                           rhs=qiT[:, :, 4 * s:4 * s + 4], start=True, stop=True), r=['kiTb', 'qiTs'], w=['ps2'])
                        Sx.op('act', lambda e: e.activation(out=rs[0:NP, :, :], in_=ps[2][0:NP, :].rearrange("p (b c) -> p b c", b=16), func=AF.Relu),
                              r=['ps2'], w=['rs'])
                        Sx.op('dve', lambda e: e.tensor_tensor(out=rs2[0:NP], in0=rs[0:NP],
                                                               in1=wbc[0:NP, s, :, :].rearrange("p h q -> p (h q)").unsqueeze(1).to_broadcast([NP, 16, 32]),
                                                               op=ALU.mult), r=['rs', 'wbc'], w=['rs2'])
                        Sx.op('dve', lambda e: e.tensor_reduce(out=scT[0:NP, s, j0:j0 + 16, :],
                                                               in_=rs2[0:NP].rearrange("p b (h q) -> p b q h", h=8), axis=AX.X, op=ALU.add),
                              r=['rs2'], w=['scT'])
                    Sx.op('pe', lambda e: e.matmul(ps[3][0:16, 0:32], lhsT=kiTn[:, 0:16], rhs=qiT[:, :, 4 * s:4 * s + 4], start=True, stop=True),
                          r=['kiTn', 'qiTs'], w=['ps3'])
                    Sx.op('act', lambda e: e.activation(out=rs[0:16, 0, :], in_=ps[3][0:16, 0:32], func=AF.Relu), r=['ps3'], w=['rs'])
                    Sx.op('dve', lambda e: e.tensor_tensor(out=rs2[0:16, 0, :], in0=rs[0:16, 0, :], in1=wbc[0:16, s, :, :].rearrange("p h q -> p (h q)"),
                                                           op=ALU.mult), r=['rs', 'wbc'], w=['rs2'])
                    Sx.op('dve', lambda e: e.tensor_reduce(out=scT[0:16, s, 128, :], in_=rs2[0:16, 0, :].rearrange("p (h q) -> p q h", h=8),
                                                           axis=AX.X, op=ALU.add), r=['rs2'], w=['scT'])
                Sx.op('dve', lambda e: e.tensor_tensor(out=scT[:, :, 128, :], in0=scT[:, :, 128, :], in1=madd[:], op=ALU.add), r=['scT', 'madd'], w=['scT'])
                if NP < 128:
                    pass
                cm = T(st, "cm", [128, 4, NKB_S, 4]); cp = T(st, "cp", [128, 16]); lo = T(st, "slo", [128, 16]); B = T(st, "sB", [128, 16])
                mid = T(st, "smid", [128, 16]); ppq = T(st, "sppq", [128, 16]); am = T(st, "sam", [128, 16])
                Sx.op('dve', lambda e: e.tensor_scalar(out=cm[:], in0=scT[:], scalar1=-1.0e4, scalar2=None, op0=ALU.max), r=['scT'], w=['cm'])
                Sx.op('dve', lambda e: e.tensor_reduce(out=am[:].rearrange("p (s q) -> p s q", s=4), in_=cm[:].rearrange("p s b q -> p s q b"),
                                                       axis=AX.X, op=ALU.max, apply_absolute_value=True), r=['cm'], w=['sam'])
                Sx.op('pe', lambda e: e.matmul(ps[7][:, 0:16], lhsT=ones_f[:, :], rhs=am[:], start=True, stop=True), r=['ones_f', 'sam'], w=['ps7'])
                Sx.op('dve', lambda e: e.tensor_copy(out=B[:], in_=ps[7][:, 0:16]), r=['ps7'], w=['sB'])
                Sx.op('dve', lambda e: e.tensor_scalar(out=lo[:], in0=B[:], scalar1=-1.0, scalar2=None, op0=ALU.mult), r=['sB'], w=['slo'])
                for it in range(0, 40):
                    f = 2.0 ** (-it)
                    Sx.op('dve', lambda e: e.scalar_tensor_tensor(out=mid[:], in0=B[:], scalar=f, in1=lo[:], op0=ALU.mult, op1=ALU.add),
                          r=['sB', 'slo'], w=['smid'])
                    Sx.op('dve', lambda e: e.tensor_tensor(out=cm[:], in0=scT[:],
                                                           in1=mid[:].rearrange("p (s q) -> p s q", s=4).unsqueeze(2).to_broadcast([128, 4, NKB_S, 4]),
                                                           op=ALU.is_ge), r=['scT', 'smid'], w=['cm'])
                    Sx.op('dve', lambda e: e.tensor_reduce(out=cp[:].rearrange("p (s q) -> p s q", s=4), in_=cm[:].rearrange("p s b q -> p s q b"),
                                                           axis=AX.X, op=ALU.add), r=['cm'], w=['cp'])
                    Sx.op('pe', lambda e: e.matmul(ps[7][:, 0:16], lhsT=ones_f[:, :], rhs=cp[:], start=True, stop=True), r=['ones_f', 'cp'], w=['ps7'])
                    Sx.op('dve', lambda e: e.tensor_scalar(out=ppq[:], in0=ps[7][:, 0:16], scalar1=TOPK_S - 0.5, scalar2=f, op0=ALU.is_ge, op1=ALU.mult),
                          r=['ps7'], w=['sppq'])
                    Sx.op('dve', lambda e: e.tensor_tensor(out=ppq[:], in0=ppq[:], in1=B[:], op=ALU.mult), r=['sppq', 'sB'], w=['sppq'])
                    Sx.op('dve', lambda e: e.tensor_tensor(out=lo[:], in0=lo[:], in1=ppq[:], op=ALU.add), r=['sppq', 'slo'], w=['slo'])
                if dbg:
                    d1 = nc.dram_tensor("dbg_scT", [128, 4 * NKB_S * 4], F32, kind="ExternalOutput").ap()
                    d2 = nc.dram_tensor("dbg_lo", [128, 16], F32, kind="ExternalOutput").ap()
                    d3 = nc.dram_tensor("dbg_kig", [128, 8192], F32, kind="ExternalOutput").ap()
                    Sx.dma('sp', lambda e: e.dma_start(out=d1[:, :], in_=scT[:].rearrange("p s b q -> p (s b q)")), r=['scT'])
                    Sx.dma('sp', lambda e: e.dma_start(out=d2[:, :], in_=lo[:]), r=['slo'])
                mTs = T(st, "mTs", [128, 4, NKB_S, 4], BF16)
                Sx.op('dve', lambda e: e.tensor_tensor(out=mTs[:], in0=scT[:],
                                                       in1=lo[:].rearrange("p (s q) -> p s q", s=4).unsqueeze(2).to_broadcast([128, 4, NKB_S, 4]),
                                                       op=ALU.is_ge), r=['scT', 'slo'], w=['mTs'])
                kg = T(st, "kg", [128, 16 * 256]); vg_ = T(st, "vgs", [128, 16 * 256])
                kgb = T(st, "kgb", [128, 16 * 256], BF16); vgx = T(st, "vgx", [128, 16, 260], BF16)
                kTb = T(st, "kTb", [64, 16, 4, 128], BF16)
                pT = T(st, "pTs", [128, 4, 256], BF16); oacc = T(st, "oaccs", [128, 4, 256]); rec = T(st, "recs", [128, 4, 256])
                obT = T(st, "obTs", [64, 8, 128], BF16)
                Sx.op('dve', lambda e: e.memset(vgx[:], 1.0), w=['vgx'])
                Sx.op('dve', lambda e: e.memset(obT[:], 0.0), w=['obT'])
                nq = 4; W = 8
                for s in range(4):
                    for j0 in range(0, 128, 16):
                        Sx.dma('pool', lambda e: e.indirect_dma_start(out=kg[0:NP, :], out_offset=None,
                                                                      in_=bass.AP(ck.tensor, 0, [[4096, NPHYS * 8], [1, 4096]]),
                                                                      in_offset=bass.IndirectOffsetOnAxis(ap=idx8[0:NP, s, j0 // 16:j0 // 16 + 1], axis=0),
                                                                      element_offset=l * NPHYS * 32768),
                               r=['idx8'], w=['kg'])
                        Sx.dma('pool', lambda e: e.indirect_dma_start(out=vg_[0:NP, :], out_offset=None,
                                                                      in_=bass.AP(cv.tensor, 0, [[4096, NPHYS * 8], [1, 4096]]),
                                                                      in_offset=bass.IndirectOffsetOnAxis(ap=idx8[0:NP, s, j0 // 16:j0 // 16 + 1], axis=0),
                                                                      element_offset=l * NPHYS * 32768),
                               r=['idx8'], w=['vgs'])
                        Sx.op('dve', lambda e: e.tensor_copy(out=kgb[0:NP, :], in_=kg[0:NP, :]), r=['kg'], w=['kgb'])
                        Sx.op('dve', lambda e: e.tensor_copy(out=vgx[0:NP, :, :].rearrange("p j (g d) -> p j g d", g=4)[:, :, :, 0:64],
                                                             in_=vg_[0:NP, :].rearrange("p (j g d) -> p j g d", j=16, g=4)), r=['vgs'], w=['vgx'])
                        for jj in range(16):
                            for g in range(4):
                                idx = jj * 4 + g
                                bnk = 6 + (idx // 8) % 2
                                Sx.op('pe', lambda e: e.transpose(out=psb[bnk][0:64, (idx % 8) * 128:(idx % 8) * 128 + NP],
                                                                  in_=kgb[0:NP, jj * 256 + g * 64:jj * 256 + (g + 1) * 64], identity=ident[0:NP, 0:NP]),
                                      r=['kgb', 'ident'], w=[f"ps{bnk}"])
                                if idx % 8 == 7:
                                    Sx.op('act', lambda e: e.copy(out=kTb[:, jj - 1:jj + 1, :, 0:NP],
                                                                  in_=psb[bnk][0:64, :].rearrange("p (j g n) -> p j g n", j=2, g=4)[:, :, :, 0:NP]),
                                          r=[f"ps{bnk}"], w=['kTb'])
                        for jj in range(16):
                            for g in range(4):
                                Sx.op('pe', lambda e: e.matmul(ps[0][0:NP, jj * 32 + g * 8:jj * 32 + g * 8 + 8], lhsT=kTb[:, jj, g, 0:NP],
                                                               rhs=qT[:, 2 * g:2 * g + 2, 4 * s:4 * s + 4], start=True, stop=True),
                                      r=['kTb', 'qTs'], w=['ps0'])
                        p16 = pT[0:NP, :, :].rearrange("p a b -> p (a b)")[:, 0:512]
                        Sx.op('act', lambda e: e.activation(out=p16, in_=ps[0][0:NP, :], func=AF.Exp, scale=0.125), r=['ps0'], w=['pTs'])
                        p5 = p16.rearrange("p (j h q) -> p j h q", j=16, h=8)
                        Sx.op('dve', lambda e: e.tensor_tensor(out=p5, in0=p5,
                                                               in1=mTs[0:NP, s, j0:j0 + 16, :].unsqueeze(2).to_broadcast([NP, 16, 8, 4]), op=ALU.mult),
                              r=['pTs', 'mTs'], w=['pTs'])
                        for jj in range(16):
                            for g in range(4):
                                Sx.op('pe', lambda e: e.matmul(ps[2 + g][0:65, 0:8], lhsT=vgx[0:NP, jj, g * 65:(g + 1) * 65],
                                                               rhs=p16[:, jj * 32 + g * 8:jj * 32 + g * 8 + 8],
                                                               start=(j0 == 0 and jj == 0), stop=False), r=['vgx', 'pTs'], w=[f"ps{2 + g}"])
                    for g in range(4):
                        Sx.op('pe', lambda e: e.matmul(ps[1][0:16, g * 8:g * 8 + 8], lhsT=kTn[:, g, 0:16], rhs=qT[:, 2 * g:2 * g + 2, 4 * s:4 * s + 4],
                                                       start=True, stop=True), r=['kTn', 'qTs'], w=['ps1'])
                    pn = pT[0:16, :, :].rearrange("p a b -> p (a b)")[:, 512:544]
                    Sx.op('act', lambda e: e.activation(out=pn, in_=ps[1][0:16, 0:32], func=AF.Exp, scale=0.125), r=['ps1'], w=['pTs'])
                    pn3 = pn.rearrange("p (h q) -> p h q", h=8)
                    Sx.op('dve', lambda e: e.tensor_tensor(out=pn3, in0=pn3, in1=mTs[0:16, s, 128, :].unsqueeze(1).to_broadcast([16, 8, 4]), op=ALU.mult),
                          r=['pTs', 'mTs'], w=['pTs'])
                    for g in range(4):
                        Sx.op('pe', lambda e: e.matmul(ps[2 + g][0:65, 0:8], lhsT=vxn[0:16, g * 65:(g + 1) * 65],
                                                       rhs=pn[:, g * 8:g * 8 + 8], start=False, stop=True), r=['vxn', 'pTs'], w=[f"ps{2 + g}"])
                    for g in range(4):
                        Sx.op('act', lambda e: e.copy(out=oacc[0:65, g, 0:W], in_=ps[2 + g][0:65, 0:W]), r=[f"ps{2 + g}"], w=['oaccs'])
                    Sx.op('dve', lambda e: e.reciprocal(out=rec[64:65, :, 0:W], in_=oacc[64:65, :, 0:W]), r=['oaccs'], w=['recs'])
                    for g in range(4):
                        bank = ps[1]; off = 128 + g * 8
                        Sx.op('pe', lambda e: e.matmul(bank[0:64, off:off + W], lhsT=ones_f[64:65, 0:64], rhs=rec[64:65, g, 0:W], start=True, stop=True),
                              r=['ones_f', 'recs'], w=["ps1"])
                    for b2 in range(2):
                        src = ps[1][0:64, 128 + b2 * 16:128 + b2 * 16 + 16].rearrange("p (g w) -> p g w", g=2)
                        dst = obT[0:64, 4 * b2:4 * b2 + 4, 4 * s:4 * s + 4].rearrange("p (g h) q -> p g h q", g=2)
                        Sx.op('dve', lambda e: e.tensor_tensor(out=dst, in0=oacc[0:64, 2 * b2:2 * b2 + 2, 0:W].rearrange("p g (h q) -> p g h q", h=2),
                                                               in1=src.rearrange("p g (h q) -> p g h q", h=2), op=ALU.mult),
                              r=['oaccs', "ps1"], w=['obT'])
                Sx.dma('sp', lambda e: e.dma_start(out=obT_s[:, :, S:S + 128], in_=obT[:]), r=['obT'], w=[f"obT_s{NTP}"])
                if dbg:
                    d4 = nc.dram_tensor("dbg_idx8", [128, 32], I32, kind="ExternalOutput").ap()
                    Sx.dma('sp', lambda e: e.dma_start(out=d4[:, :], in_=idx8[:].rearrange("p a b -> p (a b)")), r=['idx8'])
                    Sx.dma('sp', lambda e: e.dma_start(out=d3[:, 0:4096], in_=kg[:]), r=['kg'])
                    Sx.dma('sp', lambda e: e.dma_start(out=d3[:, 4096:8192], in_=vg_[:]), r=['vgs'])
            Sx.barrier()

        def phase_merge(l):
            with ExitStack() as st:
                wA = T(st, "wA", [128, 4, DM], BF16); wB = T(st, "wB", [64, 8, DM], BF16); wO = T(st, "wO", [128, 8, DM], BF16)
                Sx.dma('pool', lambda e: e.dma_start(out=wA[:], in_=w_branch_a[l].rearrange("(c p) n -> p c n", p=128)), w=['wA'])
                Sx.dma('pool', lambda e: e.dma_start(out=wB[:], in_=w_branch_b[l].rearrange("(c p) n -> p c n", p=64)), w=['wB'])
                Sx.dma('pool', lambda e: e.dma_start(out=wO[:], in_=w_out[l].rearrange("(c p) n -> p c n", p=128)), w=['wO'])
                modt = load_mod(st, l, 1)
                xt = T(st, "xt", [128, DM]); oAb = T(st, "oAb", [128, 512], BF16); oAT = T(st, "oAT", [128, 4, 128], BF16)
                obT = T(st, "obTm", [64, 8, 128], BF16); gab = T(st, "gab", [128, 2048], BF16)
                t1 = T(st, "t1", [128, DM]); t2 = T(st, "t2", [128, DM]); mb = T(st, "mb", [128, DM], BF16)
                mTt = T(st, "mTt", [128, 8, 128], BF16); rtmp = T(st, "rtmp", [128, DM])
                for t in range(NT):
                    nt, r0 = tile_rows(t)
                    nm = "p" if t < NTP else "s"
                    src, key = x_src(l, t, 1)
                    Sx.dma('sp', lambda e: e.dma_start(out=xt[0:nt, :], in_=src), r=[key], w=['xt'])
                    Sx.dma('sp', lambda e: e.dma_start(out=oAb[0:nt, :], in_=oA_s[r0:r0 + nt, :]), r=[f"oA_s{t}"], w=['oAb'])
                    Sx.dma('sp', lambda e: e.dma_start(out=obT[:, :, 0:nt], in_=obT_s[:, :, r0:r0 + nt]), r=[f"obT_s{t}"], w=['obTm'])
                    Sx.dma('sp', lambda e: e.dma_start(out=gab[0:nt, :], in_=gab_s[r0:r0 + nt, :]), r=[f"gab_s{t}"], w=['gab'])
                    for c in range(4):
                        Sx.op('pe', lambda e: e.transpose(out=psb[7][:, c * 128:c * 128 + nt], in_=oAb[0:nt, c * 128:(c + 1) * 128],
                                                          identity=ident[0:nt, 0:nt]), r=['oAb', 'ident'], w=['ps7'])
                    Sx.op('act', lambda e: e.copy(out=oAT[:, :, 0:nt], in_=psb[7][:, 0:512].rearrange("p (c n) -> p c n", c=4)[:, :, 0:nt]),
                          r=['ps7'], w=['oAT'])
                    for hh in range(2):
                        for c in range(4):
                            Sx.op('pe', lambda e: e.matmul(ps[hh][0:nt, :], lhsT=oAT[:, c, 0:nt], rhs=wA[:, c, hh * 512:(hh + 1) * 512],
                                                           start=(c == 0), stop=(c == 3)), r=['oAT', 'wA'], w=[f"ps{hh}"])
                        for h in range(8):
                            Sx.op('pe', lambda e: e.matmul(ps[2 + hh][0:nt, :], lhsT=obT[:, h, 0:nt], rhs=wB[:, h, hh * 512:(hh + 1) * 512],
                                                           start=(h == 0), stop=(h == 7)), r=['obTm', 'wB'], w=[f"ps{2 + hh}"])
                        cs = slice(hh * 512, (hh + 1) * 512)
                        Sx.op('dve', lambda e: e.tensor_tensor(out=t1[0:nt, cs], in0=ps[hh][0:nt, :], in1=gab[0:nt, hh * 512:(hh + 1) * 512], op=ALU.mult),
                              r=[f"ps{hh}", 'gab'], w=['t1'])
                        Sx.op('dve', lambda e: e.tensor_tensor(out=t2[0:nt, cs], in0=ps[2 + hh][0:nt, :], in1=gab[0:nt, 1024 + hh * 512:1024 + (hh + 1) * 512],
                                                               op=ALU.mult), r=[f"ps{2 + hh}", 'gab'], w=['t2'])
                        Sx.op('dve', lambda e: e.tensor_tensor(out=mb[0:nt, cs], in0=t1[0:nt, cs], in1=t2[0:nt, cs], op=ALU.add), r=['t1', 't2'], w=['mb'])
                    for c in range(8):
                        Sx.op('pe', lambda e: e.transpose(out=psb[7][:, c * 128:c * 128 + nt], in_=mb[0:nt, c * 128:(c + 1) * 128],
                                                          identity=ident[0:nt, 0:nt]), r=['mb', 'ident'], w=['ps7'])
                    Sx.op('act', lambda e: e.copy(out=mTt[:, :, 0:nt], in_=psb[7][:, :].rearrange("p (c n) -> p c n", c=8)[:, :, 0:nt]),
                          r=['ps7'], w=['mTt'])
                    for hh in range(2):
                        for c in range(8):
                            Sx.op('pe', lambda e: e.matmul(ps[4 + hh][0:nt, :], lhsT=mTt[:, c, 0:nt], rhs=wO[:, c, hh * 512:(hh + 1) * 512],
                                                           start=(c == 0), stop=(c == 7)), r=['mTt', 'wO'], w=[f"ps{4 + hh}"])
                    m, gsc = modt[nm]
                    resid_update(l, t, 2, xt, 'xt', [ps[4], ps[5]], m[:, 2, :], nt, 1.0, rtmp)
            Sx.barrier()

        for l in range(DEPTH):
            phase_mod(l)
            phase_ffn(l, 0)
            phase_proj(l)
            phase_attn_prompt(l)
            phase_attn_sample(l)
            if dbg:
                break
            phase_merge(l)
            phase_ffn(l, 1)
        Sx.barrier(issuers=('sp',))
    return nc


def _consts(S, past):
    half = 8
    freqs = 500000.0 ** (-np.arange(half, dtype=np.float32) * 2.0 / 16)
    def tab(pos):
        ang = pos.astype(np.float32)[:, None] * freqs[None, :]
        return np.cos(ang).astype(np.float32), np.sin(ang).astype(np.float32)
    cp, sp = tab(np.arange(S))
    cs, ss = tab(past + (np.arange(16) % 4))
    q = np.arange(128)
    tri = np.where(q[None, :] <= q[:, None], 0.0, NEG).astype(np.float32)
    triu = (q[:, None] <= q[None, :]).astype(np.float32)
    a = np.arange(16)
    bd = ((a[:, None] // 4 == a[None, :] // 4) & (a[:, None] % 4 <= a[None, :] % 4)).astype(np.float32)
    madd = np.full((128, 4, 4), NEG, np.float32)
    for p in range(16):
        for s in range(4):
            for qq in range(4):
                if p // 4 == s and p % 4 <= qq:
                    madd[p, s, qq] = 0.0
    return dict(c_ident=np.eye(128, dtype=np.float32), c_cosp=cp, c_sinp=sp, c_coss=cs, c_sins=ss,
                c_tri=tri, c_triu=triu, c_bd=bd, c_madd=madd)


_CACHE = {}


def kernel(x_prompt, x_sample, c_prompt, c_sample, cache_k, cache_v, cache_idx_k, page_table,
           mod_w, mod_b, norm_g, ffn_w_in, ffn_w_out, w_in, sgu_norm_g, sgu_w, sgu_b,
           q_norm_g, k_norm_g, w_branch_a, w_branch_b, w_out):
    f = lambda a: np.ascontiguousarray(np.asarray(a, dtype=np.float32))
    x_prompt = f(x_prompt); x_sample = f(x_sample); c_prompt = f(c_prompt); c_sample = f(c_sample)
    B, S, _ = x_prompt.shape
    DB = x_sample.shape[0]
    NPHYS = cache_k.shape[1]
    page_table = np.asarray(page_table, dtype=np.int32)
    NPG = page_table.shape[1]
    assert B == 2 and DB == 32 and x_sample.shape[1] == 4
    key = (S, NPG, NPHYS)
    if key not in _CACHE:
        _CACHE[key] = build(S, NPG, NPHYS)
    nc = _CACHE[key]
    ck = f(cache_k).reshape(DEPTH, NPHYS, 128 * 256); cv = f(cache_v).reshape(DEPTH, NPHYS, 128 * 256)
    cik = f(cache_idx_k).reshape(DEPTH, NPHYS, 128 * 64)
    shared = dict(ck=ck, cv=cv, cik=cik, mod_w=f(mod_w), mod_b=f(mod_b), norm_g=f(norm_g), ffn_w_in=f(ffn_w_in),
                  ffn_w_out=f(ffn_w_out), w_in=f(w_in), sgu_norm_g=f(sgu_norm_g), sgu_w=f(sgu_w), sgu_b=f(sgu_b),
                  q_norm_g=f(q_norm_g), k_norm_g=f(k_norm_g), w_branch_a=f(w_branch_a), w_branch_b=f(w_branch_b), w_out=f(w_out))
    shared.update(_consts(S, NPG * 128))
    in_maps = []
    for c in range(8):
        b = c % 2
        ptT = np.zeros((128, 4), np.int32)
        ptT[:NPG, :] = page_table[4 * c:4 * c + 4, :].T
        m = dict(shared)
        m.update(xp=x_prompt[b], xs=np.ascontiguousarray(x_sample[4 * c:4 * c + 4].reshape(16, DM)),
                 cc=np.ascontiguousarray(np.concatenate([c_prompt[b:b + 1], c_sample[4 * c:4 * c + 4]], 0)), ptT=ptT)
        in_maps.append(m)
    res = run_bass_kernel_spmd(nc, in_maps, core_ids=list(range(8))).results
    y_prompt = np.stack([res[0]["yp"], res[1]["yp"]], 0)
    y_sample = np.concatenate([res[c]["ys"].reshape(4, 4, DM) for c in range(8)], 0)
    def pst(nm, d):
        return np.stack([res[0][nm], res[1][nm]], 1).reshape(DEPTH, 2, S, *d)
    def sst(nm, d):
        return np.concatenate([res[c][nm].reshape(DEPTH, 4, 4, *d) for c in range(8)], 1)
    return (y_prompt.astype(np.float32), y_sample.astype(np.float32),
            pst("nkp", (4, 64)), pst("nvp", (4, 64)), pst("nip", (64,)),
            sst("nks", (4, 64)), sst("nvs", (4, 64)), sst("nis", (64,)), sst("ncv", (512,)))
```
